# Optimizing a Trainium2 kernel written in Bass

```python
import jax, jax.numpy as jnp
from jax import lax
import numpy as np

D_MODEL = 4096
BATCH = 2
SEQ = 8192
DEPTH = 2

GRID_W = 64
CTX_LEN = 256
CONV_DIM = 1024
CONV_WIDTH = 3
MLA_HEADS = 16
QK_NOPE_DIM = 128
QK_ROPE_DIM = 64
V_HEAD_DIM = 128
Q_LORA_RANK = 1024
KV_LORA_RANK = 512
ROPE_THETA = 10000.0
Q_BLOCK = 128
POOL_WINDOWS = (2, 4, 8, 16)
POOL_GROUPS = 4
POOL_GROUP_DIM = 256
POOL_DIM = POOL_GROUPS * POOL_GROUP_DIM
N_BRANCHES = 3
N_MOD = 6
D_FF = (8 * D_MODEL + 3 * 256 - 1) // (3 * 256) * 256
NORM_EPS = 1e-6

COL_CONV = 0
COL_Q = COL_CONV + 3 * CONV_DIM
COL_KV = COL_Q + Q_LORA_RANK
COL_KPE = COL_KV + KV_LORA_RANK
COL_POOL = COL_KPE + QK_ROPE_DIM
COL_GATE = COL_POOL + POOL_DIM
IN_DIM = COL_GATE + N_BRANCHES * D_MODEL

kernel_name = 'hybrid_gated_conv_mla_pool_dit'


def rmsnorm(x, w):
    xf = x.astype(jnp.float32)
    y = xf * lax.rsqrt(jnp.mean(xf * xf, axis=-1, keepdims=True) + NORM_EPS)
    return (y * w.astype(jnp.float32)).astype(x.dtype)


def modulate(x, norm_w, shift, scale):
    return rmsnorm(x, norm_w) * (1 + scale) + shift


def axial_rope_tables(n_tokens, dtype):
    rows = n_tokens // GRID_W
    row = jnp.repeat(jnp.arange(rows, dtype=jnp.int32), GRID_W)
    col = jnp.broadcast_to(jnp.arange(GRID_W, dtype=jnp.int32)[None, :], (rows, GRID_W)).reshape(-1)
    axis_dim = QK_ROPE_DIM // 2
    inv_freq = 1.0 / (ROPE_THETA ** (jnp.arange(0, axis_dim, 2, dtype=jnp.float32) / axis_dim))
    ang = jnp.stack([row.astype(jnp.float32)[:, None] * inv_freq,
                     col.astype(jnp.float32)[:, None] * inv_freq], axis=1)
    return jnp.cos(ang).astype(dtype), jnp.sin(ang).astype(dtype)


def apply_axial_rope(x, cos, sin):
    half = QK_ROPE_DIM // 4
    xs = x.reshape(*x.shape[:-1], 2, 2, half)
    x1, x2 = xs[..., 0, :], xs[..., 1, :]
    bshape = (cos.shape[0],) + (1,) * (x.ndim - 3) + (2, half)
    cs, sn = cos.reshape(bshape), sin.reshape(bshape)
    out = jnp.stack([x1 * cs - x2 * sn, x1 * sn + x2 * cs], axis=-2)
    return out.reshape(x.shape)


def mla_queries(cq, q_norm_w, w_uq, rope):
    b, s, _ = cq.shape
    q = (rmsnorm(cq, q_norm_w) @ w_uq).reshape(b, s, MLA_HEADS, QK_NOPE_DIM + QK_ROPE_DIM)
    q_nope, q_pe = q[..., :QK_NOPE_DIM], q[..., QK_NOPE_DIM:]
    if rope is not None:
        q_pe = apply_axial_rope(q_pe, *rope)
    return q_nope, q_pe


def mla_keys_values(ckv, k_pe, kv_norm_w, w_ukv, rope):
    b, s, _ = ckv.shape
    kv = (rmsnorm(ckv, kv_norm_w) @ w_ukv).reshape(b, s, MLA_HEADS, QK_NOPE_DIM + V_HEAD_DIM)
    k_nope, v = kv[..., :QK_NOPE_DIM], kv[..., QK_NOPE_DIM:]
    if rope is not None:
        k_pe = apply_axial_rope(k_pe, *rope)
    return k_nope, k_pe, v


def mla_attend(q_nope, q_pe, k_nope, k_pe, v):
    scale = (QK_NOPE_DIM + QK_ROPE_DIM) ** -0.5
    s = (jnp.einsum('bqhd,bkhd->bhqk', q_nope, k_nope)
         + jnp.einsum('bqhr,bkr->bhqk', q_pe, k_pe))
    p = jax.nn.softmax(s.astype(jnp.float32) * scale, axis=-1).astype(v.dtype)
    return jnp.einsum('bhqk,bkhd->bqhd', p, v)


def blocked_attention(q_nope, q_pe, k_nope, k_pe, v):
    b, s, h, _ = q_nope.shape
    nb = s // Q_BLOCK

    def to_blocks(t):
        return t.reshape(b, nb, Q_BLOCK, *t.shape[2:]).swapaxes(0, 1)

    out = lax.map(lambda qs: mla_attend(qs[0], qs[1], k_nope, k_pe, v),
                  (to_blocks(q_nope), to_blocks(q_pe)))
    return out.swapaxes(0, 1).reshape(b, s, h * V_HEAD_DIM)


def short_conv_branch(f_conv, conv_w):
    bg = f_conv[..., :CONV_DIM]
    cg = f_conv[..., CONV_DIM:2 * CONV_DIM]
    xin = f_conv[..., 2 * CONV_DIM:]
    u = cg * xin
    pad = CONV_WIDTH // 2
    s = u.shape[1]
    up = jnp.pad(u, ((0, 0), (pad, pad), (0, 0)))
    y = sum(up[:, k:k + s] * conv_w[k] for k in range(CONV_WIDTH))
    return bg * y


def pool_branch(u, pool_w, pool_scale):
    b, s, _ = u.shape
    g = u.reshape(b, s, POOL_GROUPS, POOL_GROUP_DIM).astype(jnp.float32)
    csum = jnp.pad(jnp.cumsum(g, axis=1), ((0, 0), (1, 0), (0, 0), (0, 0)))
    t = jnp.arange(s, dtype=jnp.int32)
    pooled = []
    for gi, w in enumerate(POOL_WINDOWS):
        lo = jnp.maximum(t - w // 2, 0)
        hi = jnp.minimum(t + (w - w // 2), s)
        cs = csum[:, :, gi]
        win_sum = cs[:, hi] - cs[:, lo]
        cnt = (hi - lo).astype(jnp.float32)[None, :, None]
        pooled.append(win_sum / cnt - g[:, :, gi])
    p = jnp.stack(pooled, axis=2).astype(u.dtype)
    y = jnp.einsum('bsgi,gio->bsgo', p, pool_w).reshape(b, s, POOL_DIM)
    return y * pool_scale


def mix_stream(f, attn_y, conv_w, pool_w, pool_scale, w_conv_out, w_mla_out, w_pool_out, w_o):
    b, s, _ = f.shape
    conv_y = short_conv_branch(f[..., COL_CONV:COL_Q], conv_w)
    pool_y = pool_branch(f[..., COL_POOL:COL_GATE], pool_w, pool_scale)
    gates = jax.nn.sigmoid(f[..., COL_GATE:]).reshape(b, s, N_BRANCHES, D_MODEL)
    merged = (gates[:, :, 0] * (conv_y @ w_conv_out)
              + gates[:, :, 1] * (attn_y @ w_mla_out)
              + gates[:, :, 2] * (pool_y @ w_pool_out))
    return merged @ w_o


def swiglu(h, w_gate, w_up, w_down):
    return (jax.nn.silu(h @ w_gate) * (h @ w_up)) @ w_down


def setup_inputs(seed: int = 0) -> dict:
    key = jax.random.key(seed)
    ks = jax.random.split(key, 26)
    L, D = DEPTH, D_MODEL

    def nrm(k, shape, scale):
        return jax.random.normal(k, shape, jnp.float32) * scale

    return {
        'x': nrm(ks[0], (BATCH, SEQ, D), 1.0),
        'c': nrm(ks[1], (BATCH, D), 1.0),
        'ctx': nrm(ks[2], (BATCH, CTX_LEN, D), 1.0),
        'c_ctx': nrm(ks[3], (D,), 1.0),
        'w_mod': nrm(ks[4], (L, D, N_MOD * D), 0.5 * D ** -0.5),
        'b_mod': nrm(ks[5], (L, N_MOD * D), 0.01),
        'norm_mix_w': 1.0 + nrm(ks[6], (L, D), 0.01),
        'norm_ffn_w': 1.0 + nrm(ks[7], (L, D), 0.01),
        'w_in': nrm(ks[8], (L, D, IN_DIM), D ** -0.5),
        'conv_w': nrm(ks[9], (L, CONV_WIDTH, CONV_DIM), CONV_WIDTH ** -0.5),
        'w_conv_out': nrm(ks[10], (L, CONV_DIM, D), CONV_DIM ** -0.5),
        'q_norm_w': 1.0 + nrm(ks[11], (L, Q_LORA_RANK), 0.01),
        'kv_norm_w': 1.0 + nrm(ks[12], (L, KV_LORA_RANK), 0.01),
        'w_uq': nrm(ks[13], (L, Q_LORA_RANK, MLA_HEADS * (QK_NOPE_DIM + QK_ROPE_DIM)), Q_LORA_RANK ** -0.5),
        'w_ukv': nrm(ks[14], (L, KV_LORA_RANK, MLA_HEADS * (QK_NOPE_DIM + V_HEAD_DIM)), KV_LORA_RANK ** -0.5),
        'w_mla_out': nrm(ks[15], (L, MLA_HEADS * V_HEAD_DIM, D), (MLA_HEADS * V_HEAD_DIM) ** -0.5),
        'pool_w': nrm(ks[16], (L, POOL_GROUPS, POOL_GROUP_DIM, POOL_GROUP_DIM), POOL_GROUP_DIM ** -0.5),
        'pool_scale': 1.0 + nrm(ks[17], (L, POOL_DIM), 0.1),
        'w_pool_out': nrm(ks[18], (L, POOL_DIM, D), POOL_DIM ** -0.5),
        'w_o': nrm(ks[19], (L, D, D), D ** -0.5),
        'w_ffn_gate': nrm(ks[20], (L, D, D_FF), D ** -0.5),
        'w_ffn_up': nrm(ks[21], (L, D, D_FF), D ** -0.5),
        'w_ffn_down': nrm(ks[22], (L, D_FF, D), D_FF ** -0.5),
        'final_norm_w': 1.0 + nrm(ks[23], (D,), 0.01),
    }


def reference(x, c, ctx, c_ctx, w_mod, b_mod, norm_mix_w, norm_ffn_w, w_in, conv_w, w_conv_out,
              q_norm_w, kv_norm_w, w_uq, w_ukv, w_mla_out, pool_w, pool_scale, w_pool_out, w_o,
              w_ffn_gate, w_ffn_up, w_ffn_down, final_norm_w):
    b, n_lat, _ = x.shape
    n_ctx = ctx.shape[1]
    rope = axial_rope_tables(n_lat, x.dtype)
    silu_c = jax.nn.silu(c)
    silu_cc = jax.nn.silu(c_ctx)
    h_ctx = ctx
    for l in range(DEPTH):
        last = l == DEPTH - 1
        mod = (silu_c @ w_mod[l] + b_mod[l]).reshape(b, 1, N_MOD, D_MODEL)
        sh1, sc1, g1, sh2, sc2, g2 = (mod[:, :, i] for i in range(N_MOD))
        mod_c = (silu_cc @ w_mod[l] + b_mod[l]).reshape(N_MOD, D_MODEL)
        csh1, csc1, cg1, csh2, csc2, cg2 = (mod_c[i] for i in range(N_MOD))
        w_in_l = w_in[l]
        mixer_w = (conv_w[l], pool_w[l], pool_scale[l], w_conv_out[l], w_mla_out[l], w_pool_out[l], w_o[l])

        hc = modulate(h_ctx, norm_mix_w[l], csh1, csc1)
        if last:
            fc_kv = hc @ w_in_l[:, COL_KV:COL_POOL]
        else:
            fc = hc @ w_in_l
            fc_kv = fc[..., COL_KV:COL_POOL]
        kn_c, kp_c, v_c = mla_keys_values(fc_kv[..., :KV_LORA_RANK], fc_kv[..., KV_LORA_RANK:],
                                          kv_norm_w[l], w_ukv[l], None)

        h = modulate(x, norm_mix_w[l], sh1, sc1)
        f = h @ w_in_l
        qn, qp = mla_queries(f[..., COL_Q:COL_KV], q_norm_w[l], w_uq[l], rope)
        kn, kp, v = mla_keys_values(f[..., COL_KV:COL_KPE], f[..., COL_KPE:COL_POOL],
                                    kv_norm_w[l], w_ukv[l], rope)
        attn = blocked_attention(qn, qp,
                                 jnp.concatenate([kn_c, kn], axis=1),
                                 jnp.concatenate([kp_c, kp], axis=1),
                                 jnp.concatenate([v_c, v], axis=1))
        x = x + g1 * mix_stream(f, attn, *mixer_w)
        x = x + g2 * swiglu(modulate(x, norm_ffn_w[l], sh2, sc2), w_ffn_gate[l], w_ffn_up[l], w_ffn_down[l])

        if not last:
            qn_c, qp_c = mla_queries(fc[..., COL_Q:COL_KV], q_norm_w[l], w_uq[l], None)
            attn_c = mla_attend(qn_c, qp_c, kn_c, kp_c, v_c).reshape(b, n_ctx, MLA_HEADS * V_HEAD_DIM)
            h_ctx = h_ctx + cg1 * mix_stream(fc, attn_c, *mixer_w)
            h_ctx = h_ctx + cg2 * swiglu(modulate(h_ctx, norm_ffn_w[l], csh2, csc2),
                                         w_ffn_gate[l], w_ffn_up[l], w_ffn_down[l])
    return rmsnorm(x, final_norm_w)
```

```python
import numpy as np
from contextlib import ExitStack
import concourse.bass as bass
import concourse.mybir as mybir
from concourse.bass_utils import run_bass_kernel_spmd

F32 = mybir.dt.float32
BF16 = mybir.dt.bfloat16
AF = mybir.ActivationFunctionType
ALU = mybir.AluOpType

NCORE = 8
CPB = 4
HALO = 8


def make_cfg(small=False):
    if small:
        c = dict(D=512, B=2, S=1024, GRID_W=64, CTX=128, CONV=256, H=2, DN=128, DR=64, DV=128,
                 QL=256, KVL=128, PG=4, PGD=128, DFF=768, DEPTH=2)
    else:
        c = dict(D=4096, B=2, S=8192, GRID_W=64, CTX=256, CONV=1024, H=16, DN=128, DR=64, DV=128,
                 QL=1024, KVL=512, PG=4, PGD=256, DFF=11008, DEPTH=2)
    c['POOL'] = c['PG'] * c['PGD']
    c['NT'] = c['S'] // CPB
    c['NC'] = c['CTX']
    c['NTOK'] = c['NT'] + c['NC']
    c['NKEY'] = c['NC'] + c['S']
    c['HDN'] = c['H'] * c['DN']
    c['HDR'] = c['H'] * c['DR']
    c['HDV'] = c['H'] * c['DV']
    c['NIN'] = 3 * c['CONV'] + c['QL'] + c['KVL'] + c['POOL'] + 3 * c['D'] + 2 * c['DR']
    c['NUQ'] = c['HDN'] + 2 * c['HDR']
    c['KCAT'] = c['CONV'] + c['HDV'] + c['POOL']
    c['MODS'] = 6 * c['D'] // NCORE
    c['EF'] = c['CONV'] + c['POOL']
    return c


def in_cols(c):
    o = {}
    o['conv'] = 0
    o['q'] = 3 * c['CONV']
    o['kv'] = o['q'] + c['QL']
    o['pool'] = o['kv'] + c['KVL']
    o['gate'] = o['pool'] + c['POOL']
    o['kpe'] = o['gate'] + 3 * c['D']
    o['kpep'] = o['kpe'] + c['DR']
    return o


def vec_layout(c):
    DC = c['D'] // 128
    items = [('nmw', DC), ('nfw', DC), ('qnw', c['QL'] // 128), ('kvnw', c['KVL'] // 128),
             ('convw', 3 * (c['CONV'] // 128)), ('pscale', c['POOL'] // 128), ('bmod', c['MODS'] // 128)]
    off = {}
    o = 0
    for l in range(c['DEPTH']):
        for n, w in items:
            off[(n, l)] = (o, w)
            o += w
    off[('fw', 0)] = (o, DC)
    o += DC
    return off, o


def tab_layout(c):
    NTOK = c['NTOK']
    items = [('cos', NTOK), ('sin', NTOK), ('invc', c['PG'] * NTOK)]
    off = {}
    o = 0
    for n, w in items:
        off[n] = (o, w)
        o += w
    return off, o


def const_layout(c):
    items = [('sel', 2 * NCORE), ('oh', 2), ('ident', 128)]
    off = {}
    o = 0
    for n, w in items:
        off[n] = (o, w)
        o += w
    return off, o


class Prog:
    ENG = ['pe', 'act', 'dve', 'sp', 'pool']
    KD = 8

    def __init__(self):
        self.ops = {e: [] for e in self.ENG}
        self.lastw = {}
        self.readers = {}
        self.ncc = 0

    def add(self, eng, fn, reads=(), writes=(), kind='c'):
        idx = len(self.ops[eng])
        deps = set()
        for r in reads:
            lw = self.lastw.get(r)
            if lw is not None:
                deps.add(lw)
        for w in writes:
            lw = self.lastw.get(w)
            if lw is not None:
                deps.add(lw)
            rd = self.readers.get(w)
            if rd is not None:
                deps.update(rd[0].items())
                deps.update(rd[1])
        op = dict(fn=fn, deps=deps, kind=kind, needed=False)
        if kind == 'cc':
            op['cc'] = self.ncc
            self.ncc += 1
        self.ops[eng].append(op)
        for r in reads:
            rd = self.readers.get(r)
            if rd is None:
                rd = ({}, set())
                self.readers[r] = rd
            if kind == 'c':
                rd[0][eng] = idx
            else:
                rd[1].add((eng, idx))
        for w in writes:
            self.lastw[w] = (eng, idx)
            self.readers[w] = ({}, set())
        return (eng, idx)

    def emit(self, nc, es):
        ENG = self.ENG
        handles = dict(pe=nc.tensor, act=nc.scalar, dve=nc.vector, sp=nc.sync, pool=nc.gpsimd)
        csem = {e: es.enter_context(nc.semaphore("c_" + e)) for e in ('pe', 'act', 'dve')}
        dsem = {e: [es.enter_context(nc.semaphore("d_%s%d" % (e, i))) for i in range(self.KD)] for e in ('sp', 'pool', 'act')}
        KCC = 16
        ccsem = [es.enter_context(nc.semaphore("cc%d" % i)) for i in range(min(KCC, max(1, self.ncc)))]
        for e in ENG:
            for i, op in enumerate(self.ops[e]):
                op['deps'].discard((e, i))
                for (de, di) in op['deps']:
                    self.ops[de][di]['needed'] = True
        for e in ('pe', 'act', 'dve'):
            cnt = 0
            for op in self.ops[e]:
                if op['kind'] != 'c':
                    continue
                if op['needed']:
                    cnt += 1
                op['tgt'] = (csem[e], cnt)
        for e in ('sp', 'pool', 'act'):
            j = 0
            for op in self.ops[e]:
                if op['kind'] == 'd':
                    op['dj'] = j
                    op['tgt'] = (dsem[e][j % self.KD], 16 * (j // self.KD + 1))
                    j += 1
                elif op['kind'] == 'cc':
                    op['tgt'] = (ccsem[op['cc'] % KCC], op['cc'] // KCC + 1)
        block = es.enter_context(nc.Block())

        def run(e):
            def body(eng):
                waited = {}
                lastj = {}
                for i, op in enumerate(self.ops[e]):
                    need = {}
                    for (de, di) in op['deps']:
                        if de == e and e == 'pe':
                            continue
                        sem, val = self.ops[de][di]['tgt']
                        k = id(sem)
                        if need.get(k, (None, 0))[1] < val:
                            need[k] = (sem, val)
                    if op['kind'] == 'd' and op['dj'] >= self.KD:
                        sem, val = op['tgt']
                        k = id(sem)
                        pv = val - 16
                        if need.get(k, (None, 0))[1] < pv:
                            need[k] = (sem, pv)
                    if op['kind'] == 'cc' and op['cc'] >= KCC:
                        sem, val = op['tgt']
                        k = id(sem)
                        pv = val - 1
                        if need.get(k, (None, 0))[1] < pv:
                            need[k] = (sem, pv)
                    for k, (sem, val) in need.items():
                        if waited.get(k, 0) >= val:
                            continue
                        eng.wait_ge(sem, val)
                        waited[k] = val
                    ins = op['fn'](eng)
                    if op['kind'] == 'd':
                        ins.then_inc(op['tgt'][0], 16)
                        lastj[id(op['tgt'][0])] = op['tgt']
                    elif op['kind'] == 'cc':
                        ins.then_inc(op['tgt'][0])
                        lastj[id(op['tgt'][0])] = op['tgt']
                    elif op['needed']:
                        ins.then_inc(op['tgt'][0], 1)
                for k, (sem, val) in lastj.items():
                    if waited.get(k, 0) < val:
                        eng.wait_ge(sem, val)
            return body
        block.tensor(run('pe'))
        block.scalar(run('act'))
        block.vector(run('dve'))
        block.sync(run('sp'))
        block.gpsimd(run('pool'))


class DT:
    def __init__(self, nc, name, shape, dtype, kind="Internal"):
        self.name = name
        self.shape = list(shape)
        self.dtype = dtype
        self.h = nc.dram_tensor(name, list(shape), dtype, kind=kind)
        self.ap = self.h.ap()

    def res(self, r0=0, r1=None):
        if r1 is None:
            r1 = self.shape[0]
        return [('dr', self.name, i) for i in range(r0 // 128, (r1 - 1) // 128 + 1)]


class WS:
    def __init__(self, nc, name, K, N):
        self.name, self.K, self.N = name, K, N
        R = K // NCORE
        ns = max(128, min(N, (400 * 1024 // (R * 2)) // 128 * 128))
        self.R = R
        self.bounds = []
        c0 = 0
        while c0 < N:
            w = min(ns, N - c0)
            self.bounds.append((c0, w))
            c0 += w
        self.gi = [DT(nc, "%s_gi%d" % (name, i), [R, w], BF16) for i, (_, w) in enumerate(self.bounds)]
        self.mid = [DT(nc, "%s_mid%d" % (name, i), [2 * R, w], BF16) for i, (_, w) in enumerate(self.bounds)]
        self.full = [DT(nc, "%s_f%d" % (name, i), [K, w], BF16) for i, (_, w) in enumerate(self.bounds)]

    def slab(self, c0):
        for i, (b0, w) in enumerate(self.bounds):
            if b0 <= c0 < b0 + w:
                return i
        raise IndexError(c0)

    def view(self, r0, r1, c0, w):
        i = self.slab(c0)
        b0, bw = self.bounds[i]
        assert c0 + w <= b0 + bw, (self.name, c0, w, b0, bw)
        f = self.full[i]
        return f.ap[r0:r1, c0 - b0:c0 - b0 + w], f.res(r0, r1)


def slab_bounds(K, N):
    R = K // NCORE
    ns = max(128, min(N, (400 * 1024 // (R * 2)) // 128 * 128))
    out, c0 = [], 0
    while c0 < N:
        w = min(ns, N - c0)
        out.append((c0, w))
        c0 += w
    return out


class Tile:
    def __init__(self, ap, pages):
        self.ap = ap
        self.res = [('sb', p) for p in pages]


class Arena:
    PAGE = 256

    def __init__(self, ap, words):
        self.base = ap
        self.words = words
        self.top = 0

    def alloc(self, shape, dtype):
        nfree = int(np.prod(shape[1:]))
        nwords = (nfree * (2 if dtype == BF16 else 4) + 3) // 4
        npages = (nwords + self.PAGE - 1) // self.PAGE
        w0 = self.top
        self.top += npages * self.PAGE
        assert self.top <= self.words, "SBUF arena overflow %d > %d" % (self.top, self.words)
        v = self.base[:, w0:w0 + nwords]
        if dtype == BF16:
            v = v.bitcast(BF16)[:, 0:nfree]
        if len(shape) > 2:
            names = ["d%d" % i for i in range(len(shape) - 1)]
            v = v.rearrange("p (%s) -> p %s" % (" ".join(names), " ".join(names)),
                            **{n: int(sz) for n, sz in zip(names, shape[1:])})
        if shape[0] < 128:
            v = v[0:shape[0]]
        return Tile(v, range(w0 // self.PAGE, w0 // self.PAGE + npages))


def fm(ap, r0, r1, t0, t1):
    return ap[r0:r1, t0:t1].rearrange("(c p) t -> p c t", p=128)


class Builder:
    def __init__(self, cfg):
        self.c = cfg
        self.nc = bass.Bass("TRN2", target_bir_lowering=False)
        self.P = Prog()
        self.bank_rr = 0
        self.uid = 0

    def dma(self, q, out, in_, reads, writes):
        self.P.add(q, lambda eng, o=out, i=in_: eng.dma_start(out=o, in_=i), reads, writes, kind='d')

    def ld(self, out_tile, out_ap, in_ap, in_res):
        self.dma('sp', out_ap, in_ap, in_res, out_tile.res)

    def st(self, out_ap, out_res, in_tile, in_ap):
        self.dma('act', out_ap, in_ap, in_tile.res, out_res)

    def op(self, eng, fn, reads, writes):
        self.P.add(eng, fn, reads, writes)

    def bank(self):
        b = self.bank_rr
        self.bank_rr = (self.bank_rr + 1) % 8
        return b

    def psb(self, b, parts=128, n=512):
        return self.ps[0:parts, b, 0:n]

    def allgather(self, gin, gout, groups):
        def fn(eng, gi=gin, go=gout, g=groups):
            return eng.collective_compute("AllGather", ALU.bypass, replica_groups=g,
                                          ins=[gi.ap.opt()], outs=[go.ap.opt()])
        self.P.add('pool', fn, gin.res(), gout.res(), kind='cc')

    def cast_copy(self, dst, src, sm0, sm1):
        CH = 2048
        tot = dst.shape[0] * dst.shape[1]
        assert tot == (sm1 - sm0) * CH, (dst.name, tot, sm0, sm1)
        M = tot // CH
        sv = src.ap.rearrange("r n -> (r n)").rearrange("(a b) -> a b", b=CH)
        dv = dst.ap.rearrange("r n -> (r n)").rearrange("(a b) -> a b", b=CH)
        pend = None
        for m0 in range(0, M, 128):
            m1 = min(M, m0 + 128)
            t = self.cstg[self.cstg_i % len(self.cstg)]
            self.cstg_i += 1
            self.dma('pool', t.ap[0:m1 - m0, :], sv[sm0 + m0:sm0 + m1, :], src.res(), t.res)
            if pend is not None:
                self.dma('pool', pend[0], pend[1], pend[2], dst.res())
            pend = (dv[m0:m1, :], t.ap[0:m1 - m0, :], t.res)
        self.dma('pool', pend[0], pend[1], pend[2], dst.res())

    def allgather8(self, gin, mid, gout):
        self.allgather(gin, mid, [[r, r + CPB] for r in range(CPB)])
        self.allgather(mid, gout, [list(range(b * CPB, (b + 1) * CPB)) for b in range(NCORE // CPB)])

    def gemm(self, A, a_r0, W, w_r0, K, nchunks, tokblocks, epi, unit=1, kgroups=None):
        KC = K // 128
        kgroups = kgroups or [list(range(KC))]
        ar = self.arena
        mark = ar.top
        cap_tok = self.ABUF // (KC * 2)
        tiles, cur, n = [], [], 0
        for (t0, tn) in tokblocks:
            if n + tn > cap_tok and cur:
                tiles.append(cur)
                cur, n = [], 0
            cur.append((t0, tn))
            n += tn
        tiles.append(cur)
        maxtok = max(sum(tn for _, tn in t) for t in tiles)
        units = [nchunks[i:i + unit] for i in range(0, len(nchunks), unit)]
        ucols = max(sum(w for _, w in u) for u in units)
        upg = max(1, (self.WSLOT // (KC * 2)) // ucols)
        groups = [units[i:i + upg] for i in range(0, len(units), upg)]
        gcols = upg * ucols
        At = ar.alloc([128, KC, maxtok], BF16)
        NS = 3
        Ws = [ar.alloc([128, KC, gcols], BF16) for _ in range(NS)]
        gi = 0
        for tile in tiles:
            off = 0
            boffs = []
            for (t0, tn) in tile:
                for k0 in range(0, KC, 8):
                    k1 = min(KC, k0 + 8)
                    self.ld(At, At.ap[:, k0:k1, off:off + tn], fm(A.ap, a_r0 + k0 * 128, a_r0 + k1 * 128, t0, t0 + tn),
                            A.res(a_r0 + k0 * 128, a_r0 + k1 * 128))
                boffs.append(off)
                off += tn
            for grp in groups:
                Wt = Ws[gi % NS]
                gi += 1
                coff = 0
                cmap = {}
                runs = []
                for u in grp:
                    for (c0, w) in u:
                        cmap[c0] = coff
                        if runs and runs[-1][0] + runs[-1][1] == c0 and (not isinstance(W, WS) or W.slab(runs[-1][0]) == W.slab(c0)):
                            runs[-1][1] += w
                        else:
                            runs.append([c0, w, coff])
                        coff += w
                for (c0, w, co) in runs:
                    for k0 in range(0, KC, 8):
                        k1 = min(KC, k0 + 8)
                        if isinstance(W, WS):
                            wap, wres = W.view(w_r0 + k0 * 128, w_r0 + k1 * 128, c0, w)
                        else:
                            wap, wres = W.ap[w_r0 + k0 * 128:w_r0 + k1 * 128, c0:c0 + w], W.res(w_r0 + k0 * 128, w_r0 + k1 * 128)
                        self.ld(Wt, Wt.ap[:, k0:k1, co:co + w], wap.rearrange("(c p) n -> p c n", p=128), wres)
                for u in grp:
                    for bi, (t0, tn) in enumerate(tile):
                        banks = [[self.bank() for _ in kgroups] for _ in u]

                        def pe(eng, u=u, banks=banks, Wt=Wt, bo=boffs[bi], tn=tn, cmap=dict(cmap)):
                            ins = None
                            for j, (c0, w) in enumerate(u):
                                for g, kg in enumerate(kgroups):
                                    for ii, kc in enumerate(kg):
                                        ins = eng.matmul(self.psb(banks[j][g], w, tn), Wt.ap[:, kc, cmap[c0]:cmap[c0] + w],
                                                         At.ap[:, kc, bo:bo + tn], start=(ii == 0), stop=(ii == len(kg) - 1))
                            return ins
                        self.op('pe', pe, Wt.res + At.res, [('ps', b) for bb in banks for b in bb])
                        epi(u, (t0, tn), banks)
        ar.top = mark

    def norm(self, src, s_r0, F, dst, d_r0, ranges, eps=1e-6):
        FC = F // 128
        ar = self.arena
        mark = ar.top
        sdt = src.dtype
        cap = max(128, min(512, (8192 // FC) // 128 * 128))
        rr = []
        for (t0, tn, a_ap, s_ap, dt0) in ranges:
            for o in range(0, tn, cap):
                rr.append((t0 + o, min(cap, tn - o), a_ap, s_ap, dt0 + o))
        ranges = rr
        maxn = max(r[1] for r in ranges)
        X = [ar.alloc([128, FC, maxn], sdt) for _ in range(2)]
        SQ = [ar.alloc([128, maxn], F32) for _ in range(3)]
        TMP = [ar.alloc([128, maxn], F32) for _ in range(3)]
        RS = ar.alloc([128, maxn], F32)
        O = [ar.alloc([128, FC, maxn], dst.dtype) for _ in range(2)]
        for ri, (t0, tn, a_ap, s_ap, dt0) in enumerate(ranges):
            Xt = X[ri % 2]
            Ot = O[ri % 2]
            for k0 in range(0, FC, 8):
                k1 = min(FC, k0 + 8)
                self.ld(Xt, Xt.ap[:, k0:k1, 0:tn], fm(src.ap, s_r0 + k0 * 128, s_r0 + k1 * 128, t0, t0 + tn),
                        src.res(s_r0 + k0 * 128, s_r0 + k1 * 128))
            b = self.bank()
            for c in range(FC):
                sq = SQ[c % 3]
                self.op('act', lambda eng, sq=sq, Xt=Xt, c=c, tn=tn: eng.activation(sq.ap[:, 0:tn], Xt.ap[:, c, 0:tn], AF.Square),
                        Xt.res, sq.res)
                self.op('pe', lambda eng, sq=sq, b=b, c=c, tn=tn: eng.matmul(self.psb(b, 128, tn), self.onesf.ap[:, :], sq.ap[:, 0:tn],
                                                                             start=(c == 0), stop=(c == FC - 1)),
                        sq.res + self.onesf.res, [('ps', b)])
            self.op('dve', lambda eng, b=b, tn=tn: eng.tensor_scalar(RS.ap[:, 0:tn], self.psb(b, 128, tn), 1.0 / F, eps, ALU.mult, ALU.add),
                    [('ps', b)], RS.res)
            self.op('act', lambda eng, tn=tn: eng.activation(RS.ap[:, 0:tn], RS.ap[:, 0:tn], AF.Sqrt), RS.res, RS.res)
            self.op('dve', lambda eng, tn=tn: eng.reciprocal(RS.ap[:, 0:tn], RS.ap[:, 0:tn]), RS.res, RS.res)
            for c in range(FC):
                tm = TMP[c % 3]
                self.op('dve', lambda eng, tm=tm, Xt=Xt, c=c, tn=tn: eng.tensor_tensor(tm.ap[:, 0:tn], Xt.ap[:, c, 0:tn], RS.ap[:, 0:tn], ALU.mult),
                        Xt.res + RS.res, tm.res)
                bias = s_ap[:, c:c + 1] if s_ap is not None else 0.0
                self.op('act', lambda eng, tm=tm, Ot=Ot, c=c, tn=tn, a_ap=a_ap, bias=bias: eng.activation(
                    Ot.ap[:, c, 0:tn], tm.ap[:, 0:tn], AF.Identity, bias=bias, scale=a_ap[:, c:c + 1]),
                    tm.res + self.pers_res, Ot.res)
            for k0 in range(0, FC, 8):
                k1 = min(FC, k0 + 8)
                self.st(fm(dst.ap, d_r0 + k0 * 128, d_r0 + k1 * 128, dt0, dt0 + tn), dst.res(d_r0 + k0 * 128, d_r0 + k1 * 128), Ot, Ot.ap[:, k0:k1, 0:tn])
        ar.top = mark

    def build(self):
        c = self.c
        nc = self.nc
        D, NT, NC_, NTOK, NKEY = c['D'], c['NT'], c['NC'], c['NTOK'], c['NKEY']
        DC = D // 128
        L = c['DEPTH']
        H, DN, DR, DV = c['H'], c['DN'], c['DR'], c['DV']
        CONV, POOL, QL, KVL, DFF = c['CONV'], c['POOL'], c['QL'], c['KVL'], c['DFF']
        CONVC, POOLC = CONV // 128, POOL // 128
        ic = in_cols(c)
        voff, NV = vec_layout(c)
        coff, NCONST = const_layout(c)
        toff, NTAB = tab_layout(c)
        self.toff = toff
        es = ExitStack()
        self.es = es
        TB = 512
        lat_blocks = [(t, min(TB, NT - t)) for t in range(0, NT, TB)]
        ctx_blocks = [(NT + t, min(TB, NC_ - t)) for t in range(0, NC_, TB)]
        all_blocks = lat_blocks + ctx_blocks

        def ext(name, shape, dt=F32):
            return DT(nc, name, shape, dt, kind="ExternalInput")
        x_in = ext("x_own", [NT, D])
        ctx_in = ext("ctx_own", [NC_, D])
        c3_in = ext("c3", [128, DC * 4])
        vecs_in = ext("vecs", [128, NV])
        consts_in = ext("consts", [128, NCONST])
        tabs_in = ext("tabs", [128, NTAB])
        self.tabs = tabs_in
        y_out = DT(nc, "y", [NT, D], F32, kind="ExternalOutput")
        wspec = {}
        for l in range(L):
            wspec[('win', l)] = (D, c['NIN'])
            wspec[('wuq', l)] = (QL, c['NUQ'])
            wspec[('wukv', l)] = (KVL, c['HDN'] + c['HDV'])
            wspec[('wpool', l)] = (POOL, c['PGD'])
            wspec[('wcat', l)] = (c['KCAT'], D)
            wspec[('wo', l)] = (D, D)
            wspec[('wgu', l)] = (D, 2 * DFF)
            wspec[('wd', l)] = (DFF, D)
        wsh, wfull = {}, {}
        shared = {}
        for (n, l), (K, N) in wspec.items():
            nm = "%s%d" % (n, l)
            shp = [(K // NCORE) * N // 2048, 2048]
            wsh[(n, l)] = ext(nm + "_sh", shp) if not c.get('FAKEW') else DT(nc, nm + "_sh", shp, F32)
            if n not in shared:
                shared[n] = WS(nc, n, K, N)
            wfull[(n, l)] = shared[n]
        wmod_sh = [ext("wmod%d_sh" % l, [D, c['MODS']]) if not c.get('FAKEW') else DT(nc, "wmod%d_sh" % l, [D, c['MODS']], F32) for l in range(L)]
        wmod_bf = [DT(nc, "wmod%d_bf" % l, [D, c['MODS']], BF16) for l in range(L)]

        def scr(name, shape, dt=BF16):
            return DT(nc, name, shape, dt)
        xT = scr("xT", [D, NTOK], F32)
        hT = scr("hT", [D, NTOK])
        f_conv = scr("f_conv", [3 * CONV, NTOK])
        f_q = scr("f_q", [QL, NTOK])
        f_kv = scr("f_kv", [KVL, NTOK])
        f_kpe = scr("f_kpe", [128, NTOK])
        f_pool = scr("f_pool", [POOL, NTOK])
        gates = scr("gates", [3 * D, NTOK])
        cqn = scr("cqn", [QL, NTOK])
        qn = scr("qn", [c['HDN'], NTOK])
        qpr = scr("qpr", [2 * c['HDR'], NTOK])
        qpe = scr("qpe", [c['HDR'], NTOK])
        KVR = ((KVL + DR + 127) // 128) * 128
        kvn = scr("kvn", [KVR, NTOK])
        KVCH = min(NT, 512)
        NKVC = NT // KVCH
        kv_gi = [scr("kv_gi%d" % i, [KVR, KVCH]) for i in range(NKVC)]
        kv_go = [scr("kv_go%d" % i, [CPB * KVR, KVCH]) for i in range(NKVC)]
        KT = scr("KT", [c['HDN'], NKEY])
        Vd = scr("Vd", [NKEY, c['HDV']])
        bcat = scr("bcat", [c['KCAT'], NTOK])
        poolp = scr("poolp", [POOL, NTOK])
        merged = hT
        hid = gates
        assert DFF <= 3 * D
        EF = c['EF']
        edge_gi = scr("edge_gi", [EF, 2 * HALO])
        edge_mid = scr("edge_mid", [2 * EF, 2 * HALO])
        edge_go = scr("edge_go", [NCORE * EF, 2 * HALO])
        csT = scr("csT", [D, 4])
        MODC = c['MODS'] // 128
        mod_gi = scr("mod_gi", [128, L * MODC * 4], F32)
        mod_mid = scr("mod_mid", [2 * 128, L * MODC * 4], F32)
        mod_go = scr("mod_go", [NCORE * 128, L * MODC * 4], F32)

        if c.get('DUMMY_MB'):
            dummy = [scr("dummy%d" % i, [128 * 128, 4096]) for i in range(c['DUMMY_MB'] // 128)]
        AW = 52000
        arena_t = es.enter_context(nc.sbuf_tensor("arena", [128, AW], F32))
        self.ps = es.enter_context(nc.psum_tensor("ps", [128, 8, 512], F32))
        ar = Arena(arena_t[:, :], AW)
        self.arena = ar
        self.ABUF = 88064
        self.WSLOT = 22528
        vecs = ar.alloc([128, NV], F32)
        cst = ar.alloc([128, NCONST], F32)
        self.onesf = ar.alloc([128, 128], F32)
        onesb = ar.alloc([128, 128], BF16)
        modv = ar.alloc([128, NCORE, L * MODC * 4], F32)
        mo = ar.alloc([128, L, 6, DC, 2], F32)
        av = ar.alloc([128, L, 2, 2, DC], F32)
        self.cstg = [ar.alloc([128, 2048], BF16) for _ in range(3)]
        self.cstg_i = 0
        self.pers_res = vecs.res + cst.res + mo.res + av.res
        pers = self.pers_res

        def V(name, l=0):
            o, w = voff[(name, l)]
            return vecs.ap[:, o:o + w]

        def Cc(name):
            o, w = coff[name]
            return cst.ap[:, o:o + w]
        self.ld(vecs, vecs.ap[:, :], vecs_in.ap[:, :], vecs_in.res())
        self.ld(cst, cst.ap[:, :], consts_in.ap[:, :], consts_in.res())
        self.op('dve', lambda eng: eng.memset(self.onesf.ap[:, :], 1.0), [], self.onesf.res)
        self.op('dve', lambda eng: eng.memset(onesb.ap[:, :], 1.0), [], onesb.res)
        pmark = ar.top

        allg = [list(range(NCORE))]
        bgroups = [list(range(b * CPB, (b + 1) * CPB)) for b in range(c['B'])]

        def prep(key):
            if c.get('SKIP_PREP'):
                return
            sh, ws = wsh[key], wfull[key]
            m0 = 0
            for i, (b0, w) in enumerate(ws.bounds):
                m = ws.R * w // 2048
                self.cast_copy(ws.gi[i], sh, m0, m0 + m)
                m0 += m
                self.allgather8(ws.gi[i], ws.mid[i], ws.full[i])

        c3 = ar.alloc([128, DC * 4], F32)
        c3s = ar.alloc([128, DC * 4], F32)
        c3b = ar.alloc([128, DC, 4], BF16)
        self.ld(c3, c3.ap[:, :], c3_in.ap[:, :], c3_in.res())
        self.op('act', lambda eng: eng.activation(c3s.ap[:, :], c3.ap[:, :], AF.Sigmoid), c3.res, c3s.res)
        self.op('dve', lambda eng: eng.tensor_tensor(c3b.ap[:, :, :].rearrange("p a b -> p (a b)"), c3.ap[:, :], c3s.ap[:, :], ALU.mult),
                c3.res + c3s.res, c3b.res)
        for k0 in range(0, DC, 8):
            k1 = min(DC, k0 + 8)
            self.st(fm(csT.ap, k0 * 128, k1 * 128, 0, 4), csT.res(k0 * 128, k1 * 128), c3b, c3b.ap[:, k0:k1, :])
        modp = ar.alloc([128, L * MODC * 4], F32)
        for l in range(L):
            self.cast_copy(wmod_bf[l], wmod_sh[l], 0, D * c['MODS'] // 2048)
        prep(('win', 0))
        prep_list = [(n_, 0) for n_ in ['wukv', 'wuq', 'wpool', 'wcat', 'wo', 'wgu', 'wd']]

        def prep_after(n_, l_):
            if l_ + 1 < L:
                prep((n_, l_ + 1))

        def prep_next(k):
            for _ in range(k):
                if prep_list:
                    prep(prep_list.pop(0))
        for l in range(L):
            def epi_mod(u, blk, banks, l=l):
                (c0, w) = u[0]
                j = c0 // 128
                b = banks[0][0]
                o = (l * MODC + j) * 4
                bia = V('bmod', l)[:, j:j + 1]
                self.op('act', lambda eng, b=b, o=o, bia=bia: eng.activation(modp.ap[:, o:o + 4], self.psb(b, 128, 4), AF.Identity,
                                                                             bias=bia, scale=1.0),
                        [('ps', b)] + pers, modp.res)
            self.gemm(csT, 0, wmod_bf[l], 0, D, [(j * 128, 128) for j in range(MODC)], [(0, 4)], epi_mod)
        self.st(mod_gi.ap[:, :], mod_gi.res(), modp, modp.ap[:, :])
        self.allgather8(mod_gi, mod_mid, mod_go)
        self.ld(modv, modv.ap[:, :, :], mod_go.ap[:, :].rearrange("(r p) f -> p r f", p=128), mod_go.res())
        oh = Cc('oh')
        for l in range(L):
            for i in range(6):
                for cc in range(DC):
                    g = i * DC + cc
                    r, j = g // MODC, g % MODC
                    o = (l * MODC + j) * 4
                    src = modv.ap[:, r, o:o + 4]
                    dst = mo.ap[:, l, i, cc, :]
                    self.op('dve', lambda eng, src=src, dst=dst: eng.tensor_scalar(dst[:, 0:1], src[:, 0:1], oh[:, 0:1], None, ALU.mult),
                            modv.res + pers, mo.res)
                    self.op('dve', lambda eng, src=src, dst=dst: eng.scalar_tensor_tensor(dst[:, 0:1], src[:, 1:2], oh[:, 1:2], dst[:, 0:1], ALU.mult, ALU.add),
                            modv.res + pers, mo.res)
                    self.op('dve', lambda eng, src=src, dst=dst: eng.tensor_copy(dst[:, 1:2], src[:, 2:3]), modv.res, mo.res)
            for wn, (nw, si) in enumerate([('nmw', 1), ('nfw', 4)]):
                for s in range(2):
                    o_ap, i_ap, w_ap = av.ap[:, l, wn, s, :], mo.ap[:, l, si, :, s], V(nw, l)
                    self.op('dve', lambda eng, o_ap=o_ap, i_ap=i_ap, w_ap=w_ap: eng.scalar_tensor_tensor(
                        o_ap, i_ap, 1.0, w_ap, ALU.add, ALU.mult),
                        mo.res + pers, av.res)
        ar.top = pmark

        def MO(l, i, s):
            return mo.ap[:, l, i, :, s]

        ident = Cc('ident')
        mark = ar.top
        XR = [ar.alloc([128, D], F32) for _ in range(2)]
        XS = [ar.alloc([128, DC, 128], F32) for _ in range(2)]
        it = 0
        for (src, n, tb) in ((x_in, NT, 0), (ctx_in, NC_, NT)):
            for r0 in range(0, n, 128):
                xr, xs = XR[it % 2], XS[it % 2]
                it += 1
                self.ld(xr, xr.ap[:, :], src.ap[r0:r0 + 128, :], src.res(r0, r0 + 128))
                for c0 in range(0, DC, 4):
                    b = self.bank()
                    c1 = min(DC, c0 + 4)

                    def pe(eng, xr=xr, b=b, c0=c0, c1=c1):
                        ins = None
                        for cc in range(c0, c1):
                            ins = eng.transpose(self.ps[:, b, (cc - c0) * 128:(cc - c0 + 1) * 128], xr.ap[:, cc * 128:(cc + 1) * 128], ident)
                        return ins
                    self.op('pe', pe, xr.res + pers, [('ps', b)])
                    self.op('dve', lambda eng, xs=xs, b=b, c0=c0, c1=c1: eng.tensor_copy(
                        xs.ap[:, c0:c1, :], self.ps[:, b, 0:(c1 - c0) * 128].rearrange("p (a b) -> p a b", a=c1 - c0, b=128)),
                        [('ps', b)], xs.res)
                for k0 in range(0, DC, 8):
                    k1 = min(DC, k0 + 8)
                    self.st(fm(xT.ap, k0 * 128, k1 * 128, tb + r0, tb + r0 + 128), xT.res(k0 * 128, k1 * 128), xs, xs.ap[:, k0:k1, :])
        ar.top = mark

        order = ['win', 'wukv', 'wuq', 'wpool', 'wcat', 'wo', 'wgu', 'wd']
        scale = float((DN + DR) ** -0.5)
        for l in range(L):
            last = (l == L - 1)
            Wl = lambda n: wfull[(n, l)]
            blocks = lat_blocks if last else all_blocks
            prep_next(2)

            rng = [(t0, tn, av.ap[:, l, 0, 0, :], MO(l, 0, 0), t0) for (t0, tn) in lat_blocks]
            rng += [(t0, tn, av.ap[:, l, 0, 1, :], MO(l, 0, 1), t0) for (t0, tn) in ctx_blocks]
            self.norm(xT, 0, D, hT, 0, rng)
            if c.get('TRUNC', 99) <= 1:
                break

            def epi_copy(dst, d_r0, c_base, act=None, sc=None):
                def epi(u, blk, banks):
                    (c0, w) = u[0]
                    (t0, tn) = blk
                    b = banks[0][0]
                    stg = self.stage_bf()
                    if act is not None:
                        self.op('act', lambda eng: eng.activation(stg.ap[0:w, 0:tn], self.psb(b, w, tn), act), [('ps', b)], stg.res)
                    elif sc is not None:
                        self.op('act', lambda eng: eng.activation(stg.ap[0:w, 0:tn], self.psb(b, w, tn), AF.Identity, bias=0.0, scale=sc), [('ps', b)], stg.res)
                    else:
                        self.op('dve', lambda eng: eng.tensor_copy(stg.ap[0:w, 0:tn], self.psb(b, w, tn)), [('ps', b)], stg.res)
                    r0 = d_r0 + (c0 - c_base)
                    self.st(dst.ap[r0:r0 + w, t0:t0 + tn], dst.res(r0, r0 + w), stg, stg.ap[0:w, 0:tn])
                return epi
            self.stage_init()
            chunks = lambda c0, n: [(c0 + j, min(128, n - (j))) for j in range(0, n, 128)]
            self.gemm(hT, 0, Wl('win'), 0, D, chunks(ic['kv'], KVL), all_blocks, epi_copy(f_kv, 0, ic['kv']))
            self.gemm(hT, 0, Wl('win'), 0, D, [(ic['kpe'], DR)], all_blocks, epi_copy(f_kpe, 0, ic['kpe']))
            self.gemm(hT, 0, Wl('win'), 0, D, [(ic['kpep'], DR)], all_blocks, epi_copy(f_kpe, 64, ic['kpep']))
            self.stage_done()

            rng = [(t0, tn, V('kvnw', l), None, t0) for (t0, tn) in all_blocks]
            self.norm(f_kv, 0, KVL, kvn, 0, rng)
            self.rope(f_kpe, 0, 64, kvn, KVL, 64, all_blocks, pers)
            for i in range(NKVC):
                self.dma('sp', kv_gi[i].ap[:, :], kvn.ap[:, i * KVCH:(i + 1) * KVCH], kvn.res(), kv_gi[i].res())
                self.allgather(kv_gi[i], kv_go[i], bgroups)
            prep_next(2)
            if c.get('TRUNC', 99) <= 2:
                break

            self.stage_init()
            self.gemm(hT, 0, Wl('win'), 0, D, chunks(ic['conv'], 3 * CONV), blocks, epi_copy(f_conv, 0, ic['conv']))
            self.gemm(hT, 0, Wl('win'), 0, D, chunks(ic['pool'], POOL), blocks, epi_copy(f_pool, 0, ic['pool']))
            self.stage_done()
            self.edges(f_conv, f_pool, edge_gi, CONV, POOL, NT)
            self.allgather8(edge_gi, edge_mid, edge_go)
            prep_next(100 if l == 0 else 0)
            if c.get('TRUNC', 99) <= 3:
                break
            self.stage_init()
            self.gemm(hT, 0, Wl('win'), 0, D, chunks(ic['q'], QL), blocks, epi_copy(f_q, 0, ic['q']))
            self.gemm(hT, 0, Wl('win'), 0, D, chunks(ic['gate'], 3 * D), blocks, epi_copy(gates, 0, ic['gate'], act=AF.Sigmoid))
            self.stage_done()
            prep_after('win', l)
            if c.get('TRUNC', 99) <= 4:
                break

            rng = [(t0, tn, V('qnw', l), None, t0) for (t0, tn) in blocks]
            self.norm(f_q, 0, QL, cqn, 0, rng)
            self.stage_init()
            self.gemm(cqn, 0, Wl('wuq'), 0, QL, chunks(0, c['HDN']), blocks, epi_copy(qn, 0, 0, sc=scale))
            self.gemm(cqn, 0, Wl('wuq'), 0, QL, chunks(c['HDN'], 2 * c['HDR']), blocks, epi_copy(qpr, 0, c['HDN'], sc=scale))
            self.stage_done()
            prep_after('wuq', l)
            for j in range(c['HDR'] // 128):
                self.rope(qpr, j * 128, c['HDR'] + j * 128, qpe, j * 128, 128, blocks, pers)
            if c.get('TRUNC', 99) <= 5:
                break

            self.kv_up(kvn, kv_go, KVR, Wl('wukv'), KT, Vd)
            prep_after('wukv', l)
            if c.get('TRUNC', 99) <= 6:
                break
            self.attention(qn, qpe, kvn, kv_go, KVR, KT, Vd, bcat, CONV, onesb, lat_blocks, [] if last else ctx_blocks)
            if c.get('TRUNC', 99) <= 7:
                break

            self.conv_branch(f_conv, edge_go, bcat, V('convw', l), Cc('sel'), pers, [(0, NT, True)] + ([] if last else [(NT, NC_, False)]))
            self.pool_branch(f_pool, edge_go, poolp, Cc('sel'), pers, [(0, NT, True)] + ([] if last else [(NT, NC_, False)]))
            if c.get('TRUNC', 99) <= 8:
                break
            self.stage_init()
            for g in range(c['PG']):
                PGD = c['PGD']

                def epi_pool(u, blk, banks, g=g):
                    (c0, w) = u[0]
                    (t0, tn) = blk
                    b = banks[0][0]
                    stg = self.stage_bf()
                    cc = (g * PGD + c0) // 128
                    psc = V('pscale', l)[:, cc:cc + 1]
                    self.op('act', lambda eng: eng.activation(stg.ap[:, 0:tn], self.psb(b, 128, tn), AF.Identity, bias=0.0, scale=psc),
                            [('ps', b)] + pers, stg.res)
                    r0 = CONV + c['HDV'] + g * PGD + c0
                    self.st(bcat.ap[r0:r0 + 128, t0:t0 + tn], bcat.res(r0, r0 + 128), stg, stg.ap[:, 0:tn])
                self.gemm(poolp, g * PGD, Wl('wpool'), g * PGD, PGD, chunks(0, PGD), blocks, epi_pool)
            self.stage_done()
            prep_after('wpool', l)

            mmark = self.arena.top
            gt = [self.arena.alloc([128, 512], BF16) for _ in range(6)]
            mt = [self.arena.alloc([128, 512], F32) for _ in range(4)]
            self.stage_init()
            cnt = [0]

            def epi_merge(u, blk, banks):
                (c0, w) = u[0]
                (t0, tn) = blk
                bs = banks[0]
                k = cnt[0]
                cnt[0] += 1
                g3 = [gt[(3 * k + i) % 6] for i in range(3)]
                for i in range(3):
                    r0 = i * D + c0
                    self.ld(g3[i], g3[i].ap[:, 0:tn], gates.ap[r0:r0 + 128, t0:t0 + tn], gates.res(r0, r0 + 128))
                m0, m1 = mt[(2 * k) % 4], mt[(2 * k + 1) % 4]
                stg = self.stage_bf()
                self.op('dve', lambda eng: eng.tensor_tensor(m0.ap[:, 0:tn], self.psb(bs[0], 128, tn), g3[0].ap[:, 0:tn], ALU.mult), [('ps', bs[0])] + g3[0].res, m0.res)
                self.op('dve', lambda eng: eng.tensor_tensor(m1.ap[:, 0:tn], self.psb(bs[1], 128, tn), g3[1].ap[:, 0:tn], ALU.mult), [('ps', bs[1])] + g3[1].res, m1.res)
                self.op('dve', lambda eng: eng.tensor_tensor(m0.ap[:, 0:tn], m0.ap[:, 0:tn], m1.ap[:, 0:tn], ALU.add), m0.res + m1.res, m0.res)
                self.op('dve', lambda eng: eng.tensor_tensor(m1.ap[:, 0:tn], self.psb(bs[2], 128, tn), g3[2].ap[:, 0:tn], ALU.mult), [('ps', bs[2])] + g3[2].res, m1.res)
                self.op('dve', lambda eng: eng.tensor_tensor(stg.ap[:, 0:tn], m0.ap[:, 0:tn], m1.ap[:, 0:tn], ALU.add), m0.res + m1.res, stg.res)
                self.st(merged.ap[c0:c0 + 128, t0:t0 + tn], merged.res(c0, c0 + 128), stg, stg.ap[:, 0:tn])
            kA, kB = CONV // 128, (CONV + c['HDV']) // 128
            kgs = [list(range(0, kA)), list(range(kA, kB)), list(range(kB, c['KCAT'] // 128))]
            self.gemm(bcat, 0, Wl('wcat'), 0, c['KCAT'], chunks(0, D), blocks, epi_merge, kgroups=kgs)
            self.stage_done()
            self.arena.top = mmark
            prep_after('wcat', l)
            if c.get('TRUNC', 99) <= 9:
                break

            def epi_res(gi_idx):
                xo = [self.arena.alloc([128, 512], F32) for _ in range(3)]
                xn = [self.arena.alloc([128, 512], F32) for _ in range(3)]
                cnt = [0]

                def epi(u, blk, banks):
                    (c0, w) = u[0]
                    (t0, tn) = blk
                    b = banks[0][0]
                    k = cnt[0]
                    cnt[0] += 1
                    s = 0 if t0 < NT else 1
                    cc = c0 // 128
                    a, o = xo[k % 3], xn[k % 3]
                    self.ld(a, a.ap[:, 0:tn], xT.ap[c0:c0 + 128, t0:t0 + tn], xT.res(c0, c0 + 128))
                    gap = MO(l, gi_idx, s)[:, cc:cc + 1]
                    self.op('dve', lambda eng: eng.scalar_tensor_tensor(o.ap[:, 0:tn], self.psb(b, 128, tn), gap,
                                                                        a.ap[:, 0:tn], ALU.mult, ALU.add),
                            [('ps', b)] + a.res + pers, o.res)
                    self.st(xT.ap[c0:c0 + 128, t0:t0 + tn], xT.res(c0, c0 + 128), o, o.ap[:, 0:tn])
                return epi
            mark = self.arena.top
            self.gemm(merged, 0, Wl('wo'), 0, D, chunks(0, D), blocks, epi_res(2))
            self.arena.top = mark
            prep_after('wo', l)
            if c.get('TRUNC', 99) <= 10:
                break

            rng = [(t0, tn, av.ap[:, l, 1, 0 if t0 < NT else 1, :], MO(l, 3, 0 if t0 < NT else 1), t0) for (t0, tn) in blocks]
            self.norm(xT, 0, D, hT, 0, rng)
            self.stage_init()
            st_f = [self.arena.alloc([128, 512], F32) for _ in range(3)]
            cnt2 = [0]

            def epi_ffn(u, blk, banks):
                (c0, w) = u[0]
                (t0, tn) = blk
                bg, bu = banks[0][0], banks[1][0]
                k = cnt2[0]
                cnt2[0] += 1
                sg = st_f[k % 3]
                sg2 = st_f[(k + 1) % 3]
                stg = self.stage_bf()
                self.op('act', lambda eng: eng.activation(sg.ap[:, 0:tn], self.psb(bg, 128, tn), AF.Sigmoid), [('ps', bg)], sg.res)
                self.op('dve', lambda eng: eng.tensor_tensor(sg.ap[:, 0:tn], sg.ap[:, 0:tn], self.psb(bg, 128, tn), ALU.mult), [('ps', bg)] + sg.res, sg.res)
                self.op('dve', lambda eng: eng.tensor_tensor(stg.ap[:, 0:tn], sg.ap[:, 0:tn], self.psb(bu, 128, tn), ALU.mult), [('ps', bu)] + sg.res, stg.res)
                r0 = c0 // 2
                self.st(hid.ap[r0:r0 + 128, t0:t0 + tn], hid.res(r0, r0 + 128), stg, stg.ap[:, 0:tn])
            self.gemm(hT, 0, Wl('wgu'), 0, D, chunks(0, 2 * DFF), blocks, epi_ffn, unit=2)
            self.stage_done()
            prep_after('wgu', l)
            mark = self.arena.top
            self.gemm(hid, 0, Wl('wd'), 0, DFF, chunks(0, D), blocks, epi_res(5))
            self.arena.top = mark
            prep_after('wd', l)

        fo, fw = voff[('fw', 0)]
        yT = scr("yT", [D, NT], F32)
        rng = [(t0, tn, vecs.ap[:, fo:fo + fw], None, t0) for (t0, tn) in lat_blocks]
        self.norm(xT, 0, D, yT, 0, rng)
        mark = ar.top
        YS = [ar.alloc([128, DC, 128], F32) for _ in range(2)]
        YR = [ar.alloc([128, D], F32) for _ in range(2)]
        it = 0
        for r0 in range(0, NT, 128):
            ys, yr = YS[it % 2], YR[it % 2]
            it += 1
            for k0 in range(0, DC, 8):
                k1 = min(DC, k0 + 8)
                self.ld(ys, ys.ap[:, k0:k1, :], fm(yT.ap, k0 * 128, k1 * 128, r0, r0 + 128), yT.res(k0 * 128, k1 * 128))
            for c0 in range(0, DC, 4):
                b = self.bank()
                c1 = min(DC, c0 + 4)

                def pe(eng, ys=ys, b=b, c0=c0, c1=c1):
                    ins = None
                    for cc in range(c0, c1):
                        ins = eng.transpose(self.ps[:, b, (cc - c0) * 128:(cc - c0 + 1) * 128], ys.ap[:, cc, :], ident)
                    return ins
                self.op('pe', pe, ys.res + pers, [('ps', b)])
                self.op('dve', lambda eng, yr=yr, b=b, c0=c0, c1=c1: eng.tensor_copy(yr.ap[:, c0 * 128:c1 * 128], self.ps[:, b, 0:(c1 - c0) * 128]),
                        [('ps', b)], yr.res)
            self.st(y_out.ap[r0:r0 + 128, :], y_out.res(r0, r0 + 128), yr, yr.ap[:, :])
        ar.top = mark
        self.P.emit(nc, es)
        es.close()
        return nc

    def stage_init(self):
        self._smark = self.arena.top
        self._stg = [self.arena.alloc([128, 512], BF16) for _ in range(4)]
        self._si = 0

    def stage_bf(self):
        t = self._stg[self._si % 4]
        self._si += 1
        return t

    def stage_done(self):
        self.arena.top = self._smark

    def rope(self, src, r_pe, r_pep, dst, d_r0, rows, blocks, pers):
        ar = self.arena
        mark = ar.top
        oc = self.toff['cos'][0]
        osn = self.toff['sin'][0]
        tabs = self.tabs
        A = [ar.alloc([128, 512], BF16) for _ in range(2)]
        B = [ar.alloc([128, 512], BF16) for _ in range(2)]
        CS = [ar.alloc([128, 512], F32) for _ in range(2)]
        SN = [ar.alloc([128, 512], F32) for _ in range(2)]
        T1 = [ar.alloc([128, 512], F32) for _ in range(2)]
        T2 = [ar.alloc([128, 512], F32) for _ in range(2)]
        O = [ar.alloc([128, 512], BF16) for _ in range(2)]
        for i, (t0, tn) in enumerate(blocks):
            a, b, t1, t2, o, cs, sn = A[i % 2], B[i % 2], T1[i % 2], T2[i % 2], O[i % 2], CS[i % 2], SN[i % 2]
            self.ld(a, a.ap[0:rows, 0:tn], src.ap[r_pe:r_pe + rows, t0:t0 + tn], src.res(r_pe, r_pe + rows))
            self.ld(b, b.ap[0:rows, 0:tn], src.ap[r_pep:r_pep + rows, t0:t0 + tn], src.res(r_pep, r_pep + rows))
            self.ld(cs, cs.ap[0:rows, 0:tn], tabs.ap[0:rows, oc + t0:oc + t0 + tn], tabs.res())
            self.ld(sn, sn.ap[0:rows, 0:tn], tabs.ap[0:rows, osn + t0:osn + t0 + tn], tabs.res())
            self.op('dve', lambda eng, a=a, t1=t1, cs=cs, tn=tn: eng.tensor_tensor(t1.ap[0:rows, 0:tn], a.ap[0:rows, 0:tn], cs.ap[0:rows, 0:tn], ALU.mult),
                    a.res + cs.res, t1.res)
            self.op('dve', lambda eng, b=b, t2=t2, sn=sn, tn=tn: eng.tensor_tensor(t2.ap[0:rows, 0:tn], b.ap[0:rows, 0:tn], sn.ap[0:rows, 0:tn], ALU.mult),
                    b.res + sn.res, t2.res)
            self.op('dve', lambda eng, o=o, t1=t1, t2=t2, tn=tn: eng.tensor_tensor(o.ap[0:rows, 0:tn], t1.ap[0:rows, 0:tn], t2.ap[0:rows, 0:tn], ALU.add),
                    t1.res + t2.res, o.res)
            self.st(dst.ap[d_r0:d_r0 + rows, t0:t0 + tn], dst.res(d_r0, d_r0 + rows), o, o.ap[0:rows, 0:tn])
        ar.top = mark

    def edges(self, f_conv, f_pool, edge_gi, CONV, POOL, NT):
        ar = self.arena
        mark = ar.top
        CC = CONV // 128
        cg = ar.alloc([128, CC, 2 * HALO], BF16)
        xi = ar.alloc([128, CC, 2 * HALO], BF16)
        u = ar.alloc([128, CC, 2 * HALO], BF16)
        for side, t0 in ((0, 0), (1, NT - HALO)):
            self.ld(cg, cg.ap[:, :, side * HALO:(side + 1) * HALO], fm(f_conv.ap, CONV, 2 * CONV, t0, t0 + HALO), f_conv.res(CONV, 2 * CONV))
            self.ld(xi, xi.ap[:, :, side * HALO:(side + 1) * HALO], fm(f_conv.ap, 2 * CONV, 3 * CONV, t0, t0 + HALO), f_conv.res(2 * CONV, 3 * CONV))
        self.op('dve', lambda eng: eng.tensor_tensor(u.ap[:, :, :], cg.ap[:, :, :], xi.ap[:, :, :], ALU.mult), cg.res + xi.res, u.res)
        self.st(fm(edge_gi.ap, 0, CONV, 0, 2 * HALO), edge_gi.res(), u, u.ap[:, :, :])
        for side, t0 in ((0, 0), (1, NT - HALO)):
            self.dma('sp', edge_gi.ap[CONV:CONV + POOL, side * HALO:(side + 1) * HALO], f_pool.ap[:, t0:t0 + HALO], f_pool.res(), edge_gi.res())
        ar.top = mark

    def halos(self, edge_go, r0, nrows, sel, pers):
        ar = self.arena
        EF = self.c['EF']
        CC = nrows // 128
        E = ar.alloc([128, NCORE, CC, 2 * HALO], BF16)
        Hh = ar.alloc([128, CC, 2 * HALO], F32)
        for r in range(NCORE):
            self.ld(E, E.ap[:, r, :, :], fm(edge_go.ap, r * EF + r0, r * EF + r0 + nrows, 0, 2 * HALO), edge_go.res())
        self.op('dve', lambda eng: eng.memset(Hh.ap[:, :, :], 0.0), [], Hh.res)
        for r in range(NCORE):
            self.op('dve', lambda eng, r=r: eng.scalar_tensor_tensor(Hh.ap[:, :, 0:HALO], E.ap[:, r, :, HALO:2 * HALO], sel[:, r:r + 1], Hh.ap[:, :, 0:HALO], ALU.mult, ALU.add),
                    E.res + Hh.res + pers, Hh.res)
            self.op('dve', lambda eng, r=r: eng.scalar_tensor_tensor(Hh.ap[:, :, HALO:2 * HALO], E.ap[:, r, :, 0:HALO], sel[:, NCORE + r:NCORE + r + 1], Hh.ap[:, :, HALO:2 * HALO], ALU.mult, ALU.add),
                    E.res + Hh.res + pers, Hh.res)
        return Hh

    def conv_branch(self, f_conv, edge_go, bcat, convw, sel, pers, streams):
        c = self.c
        CONV = c['CONV']
        CC = CONV // 128
        ar = self.arena
        mark = ar.top
        Hh = self.halos(edge_go, 0, CONV, sel, pers)
        maxn = max(s[1] for s in streams)
        BG = [ar.alloc([128, maxn], BF16) for _ in range(2)]
        CG = [ar.alloc([128, maxn], BF16) for _ in range(2)]
        XI = [ar.alloc([128, maxn], BF16) for _ in range(2)]
        U = [ar.alloc([128, maxn + 2], F32) for _ in range(2)]
        T = [ar.alloc([128, maxn], F32) for _ in range(2)]
        O = [ar.alloc([128, maxn], BF16) for _ in range(2)]
        i = 0
        for (t0, n, has_halo) in streams:
            for cc in range(CC):
                bg, cg, xi, u, t, o = BG[i % 2], CG[i % 2], XI[i % 2], U[i % 2], T[i % 2], O[i % 2]
                i += 1
                for (tl, rr) in ((bg, 0), (cg, CONV), (xi, 2 * CONV)):
                    self.ld(tl, tl.ap[:, 0:n], f_conv.ap[rr + cc * 128:rr + (cc + 1) * 128, t0:t0 + n], f_conv.res(rr + cc * 128, rr + (cc + 1) * 128))
                self.op('dve', lambda eng, u=u, cg=cg, xi=xi, n=n: eng.tensor_tensor(u.ap[:, 1:n + 1], cg.ap[:, 0:n], xi.ap[:, 0:n], ALU.mult), cg.res + xi.res, u.res)
                if has_halo:
                    self.op('dve', lambda eng, u=u, cc=cc: eng.tensor_copy(u.ap[:, 0:1], Hh.ap[:, cc, HALO - 1:HALO]), Hh.res, u.res)
                    self.op('dve', lambda eng, u=u, cc=cc, n=n: eng.tensor_copy(u.ap[:, n + 1:n + 2], Hh.ap[:, cc, HALO:HALO + 1]), Hh.res, u.res)
                else:
                    self.op('dve', lambda eng, u=u: eng.memset(u.ap[:, 0:1], 0.0), [], u.res)
                    self.op('dve', lambda eng, u=u, n=n: eng.memset(u.ap[:, n + 1:n + 2], 0.0), [], u.res)
                w = lambda k, cc=cc: convw[:, cc * 3 + k:cc * 3 + k + 1]
                self.op('dve', lambda eng, u=u, t=t, n=n, w=w: eng.tensor_scalar(t.ap[:, 0:n], u.ap[:, 0:n], w(0), None, ALU.mult), u.res + pers, t.res)
                self.op('dve', lambda eng, u=u, t=t, n=n, w=w: eng.scalar_tensor_tensor(t.ap[:, 0:n], u.ap[:, 1:n + 1], w(1), t.ap[:, 0:n], ALU.mult, ALU.add), u.res + t.res + pers, t.res)
                self.op('dve', lambda eng, u=u, t=t, n=n, w=w: eng.scalar_tensor_tensor(t.ap[:, 0:n], u.ap[:, 2:n + 2], w(2), t.ap[:, 0:n], ALU.mult, ALU.add), u.res + t.res + pers, t.res)
                self.op('dve', lambda eng, o=o, t=t, bg=bg, n=n: eng.tensor_tensor(o.ap[:, 0:n], t.ap[:, 0:n], bg.ap[:, 0:n], ALU.mult), t.res + bg.res, o.res)
                self.st(bcat.ap[cc * 128:(cc + 1) * 128, t0:t0 + n], bcat.res(cc * 128, (cc + 1) * 128), o, o.ap[:, 0:n])
        ar.top = mark

    def pool_branch(self, f_pool, edge_go, poolp, sel, pers, streams):
        c = self.c
        POOL, PGD, NTOK = c['POOL'], c['PGD'], c['NTOK']
        CPG = PGD // 128
        ar = self.arena
        mark = ar.top
        Hh = self.halos(edge_go, c['CONV'], POOL, sel, pers)
        maxn = max(s[1] for s in streams)
        Ex = maxn + 2 * HALO
        PI = [ar.alloc([128, maxn], BF16) for _ in range(2)]
        E = [ar.alloc([128, Ex], F32) for _ in range(2)]
        S = [ar.alloc([128, Ex], F32) for _ in range(2)]
        S2 = [ar.alloc([128, Ex], F32) for _ in range(2)]
        O = [ar.alloc([128, maxn], BF16) for _ in range(2)]
        IV = [ar.alloc([128, maxn], F32) for _ in range(2)]
        oiv = self.toff['invc'][0]
        i = 0
        ivi = 0
        for (t0, n, has_halo) in streams:
            En = n + 2 * HALO
            for g in range(c['PG']):
                nsteps = g + 1
                iv = IV[ivi % 2]
                ivi += 1
                self.ld(iv, iv.ap[:, 0:n], self.tabs.ap[:, oiv + g * NTOK + t0:oiv + g * NTOK + t0 + n], self.tabs.res())
                for j in range(CPG):
                    cc = g * CPG + j
                    pi, e, s, s2, o = PI[i % 2], E[i % 2], S[i % 2], S2[i % 2], O[i % 2]
                    i += 1
                    self.ld(pi, pi.ap[:, 0:n], f_pool.ap[cc * 128:(cc + 1) * 128, t0:t0 + n], f_pool.res(cc * 128, (cc + 1) * 128))
                    self.op('dve', lambda eng, e=e, pi=pi, n=n: eng.tensor_copy(e.ap[:, HALO:HALO + n], pi.ap[:, 0:n]), pi.res, e.res)
                    if has_halo:
                        self.op('dve', lambda eng, e=e, cc=cc: eng.tensor_copy(e.ap[:, 0:HALO], Hh.ap[:, cc, 0:HALO]), Hh.res, e.res)
                        self.op('dve', lambda eng, e=e, cc=cc, n=n: eng.tensor_copy(e.ap[:, HALO + n:2 * HALO + n], Hh.ap[:, cc, HALO:2 * HALO]), Hh.res, e.res)
                    else:
                        self.op('dve', lambda eng, e=e: eng.memset(e.ap[:, 0:HALO], 0.0), [], e.res)
                        self.op('dve', lambda eng, e=e, n=n: eng.memset(e.ap[:, HALO + n:2 * HALO + n], 0.0), [], e.res)
                    cur, lo, hi = e, 0, En
                    bufs = [s, s2]
                    for k in range(nsteps):
                        nxt = bufs[k % 2]
                        if k == 0:
                            nlo, nhi = lo + 1, hi
                            a0, b0 = nlo - 1, nlo
                        else:
                            sh_ = 1 << (k - 1)
                            nlo, nhi = lo + sh_, hi - sh_
                            a0, b0 = nlo - sh_, nlo + sh_
                        ln = nhi - nlo
                        self.op('dve', lambda eng, nxt=nxt, cur=cur, nlo=nlo, ln=ln, a0=a0, b0=b0: eng.tensor_tensor(
                            nxt.ap[:, nlo:nlo + ln], cur.ap[:, a0:a0 + ln], cur.ap[:, b0:b0 + ln], ALU.add), cur.res, nxt.res)
                        cur, lo, hi = nxt, nlo, nhi
                    assert lo <= HALO and hi >= HALO + n
                    tmp = bufs[nsteps % 2]
                    self.op('dve', lambda eng, tmp=tmp, cur=cur, n=n, iv=iv: eng.tensor_tensor(
                        tmp.ap[:, 0:n], cur.ap[:, HALO:HALO + n], iv.ap[:, 0:n], ALU.mult), cur.res + iv.res, tmp.res)
                    self.op('dve', lambda eng, o=o, tmp=tmp, e=e, n=n: eng.tensor_tensor(o.ap[:, 0:n], tmp.ap[:, 0:n], e.ap[:, HALO:HALO + n], ALU.subtract),
                            tmp.res + e.res, o.res)
                    self.st(poolp.ap[cc * 128:(cc + 1) * 128, t0:t0 + n], poolp.res(cc * 128, (cc + 1) * 128), o, o.ap[:, 0:n])
        ar.top = mark

    def load_kvall(self, T, rows, kvn, kv_go, KVR, r0):
        c = self.c
        NT, NC_ = c['NT'], c['NC']
        self.ld(T, T.ap[0:rows, 0:NC_], kvn.ap[r0:r0 + rows, NT:NT + NC_], kvn.res(r0, r0 + rows))
        for r in range(CPB):
            for i, go in enumerate(kv_go):
                w = go.shape[1]
                self.ld(T, T.ap[0:rows, NC_ + r * NT + i * w:NC_ + r * NT + (i + 1) * w], go.ap[r * KVR + r0:r * KVR + r0 + rows, :], go.res())

    def kv_up(self, kvn, kv_go, KVR, W, KT, Vd):
        c = self.c
        KVL, NKEY, H = c['KVL'], c['NKEY'], c['H']
        KC = KVL // 128
        HDN, HDV = c['HDN'], c['HDV']
        ar = self.arena
        mark = ar.top
        A = [ar.alloc([128, NKEY], BF16) for _ in range(KC)]
        for k in range(KC):
            self.load_kvall(A[k], 128, kvn, kv_go, KVR, k * 128)
        Ares = [r for a in A for r in a.res]
        Wt = ar.alloc([128, KC, HDN + HDV], BF16)
        for k in range(KC):
            for (b0, bw) in W.bounds:
                wap, wres = W.view(k * 128, (k + 1) * 128, b0, bw)
                self.ld(Wt, Wt.ap[:, k, b0:b0 + bw], wap, wres)
        KS = [ar.alloc([128, NKEY], BF16) for _ in range(2)]
        kblocks = [(t, min(512, NKEY - t)) for t in range(0, NKEY, 512)]
        for h in range(H):
            ks = KS[h % 2]
            for (t0, tn) in kblocks:
                b = self.bank()

                def pe(eng, b=b, h=h, t0=t0, tn=tn):
                    ins = None
                    for k in range(KC):
                        ins = eng.matmul(self.psb(b, 128, tn), Wt.ap[:, k, h * 128:(h + 1) * 128], A[k].ap[:, t0:t0 + tn], start=(k == 0), stop=(k == KC - 1))
                    return ins
                self.op('pe', pe, Wt.res + Ares, [('ps', b)])
                self.op('dve', lambda eng, ks=ks, b=b, t0=t0, tn=tn: eng.tensor_copy(ks.ap[:, t0:t0 + tn], self.psb(b, 128, tn)), [('ps', b)], ks.res)
            self.st(KT.ap[h * 128:(h + 1) * 128, :], KT.res(h * 128, (h + 1) * 128), ks, ks.ap[:, :])
        VS = [ar.alloc([128, HDV], BF16) for _ in range(2)]
        for kt in range(NKEY // 128):
            vs = VS[kt % 2]
            for c0 in range(0, HDV, 512):
                cn = min(512, HDV - c0)
                b = self.bank()

                def pe(eng, b=b, kt=kt, c0=c0, cn=cn):
                    ins = None
                    for k in range(KC):
                        ins = eng.matmul(self.psb(b, 128, cn), A[k].ap[:, kt * 128:(kt + 1) * 128], Wt.ap[:, k, HDN + c0:HDN + c0 + cn], start=(k == 0), stop=(k == KC - 1))
                    return ins
                self.op('pe', pe, Wt.res + Ares, [('ps', b)])
                self.op('act', lambda eng, vs=vs, b=b, c0=c0, cn=cn: eng.activation(vs.ap[:, c0:c0 + cn], self.psb(b, 128, cn), AF.Identity), [('ps', b)], vs.res)
            self.st(Vd.ap[kt * 128:(kt + 1) * 128, :], Vd.res(kt * 128, (kt + 1) * 128), vs, vs.ap[:, :])
        ar.top = mark

    def attention(self, qn, qpe, kvn, kv_go, KVR, KT, Vd, bcat, orow, onesb, lat_blocks, ctx_blocks):
        c = self.c
        H, NKEY, NTOK, NC_, KVL, DR = c['H'], c['NKEY'], c['NTOK'], c['NC'], c['KVL'], c['DR']
        NKT = NKEY // 128
        ar = self.arena
        mark = ar.top
        kpe = ar.alloc([64, NKEY], BF16)
        self.load_kvall(kpe, DR, kvn, kv_go, KVR, KVL)
        KH = [ar.alloc([128, NKEY], BF16) for _ in range(2)]
        VH = [ar.alloc([128, NKT, 128], BF16) for _ in range(2)]
        QN = [ar.alloc([128, NTOK], BF16) for _ in range(2)]
        QP = [ar.alloc([64, NTOK], BF16) for _ in range(2)]
        PT = [ar.alloc([128, 512], BF16) for _ in range(4)]
        RC = [ar.alloc([128, 512], F32) for _ in range(2)]
        OS = [ar.alloc([128, 512], BF16) for _ in range(2)]
        SB = [0, 1, 2]
        OB = [3, 4]
        LB = [5, 6]
        si = 0
        qi = 0
        for h in range(H):
            kh, vh, qnh, qph = KH[h % 2], VH[h % 2], QN[h % 2], QP[h % 2]
            self.ld(kh, kh.ap[:, :], KT.ap[h * 128:(h + 1) * 128, :], KT.res(h * 128, (h + 1) * 128))
            for k0 in range(0, NKT, 8):
                k1 = min(NKT, k0 + 8)
                self.ld(vh, vh.ap[:, k0:k1, :], Vd.ap[k0 * 128:k1 * 128, h * 128:(h + 1) * 128].rearrange("(k p) d -> p k d", p=128), Vd.res(k0 * 128, k1 * 128))
            self.ld(qnh, qnh.ap[:, :], qn.ap[h * 128:(h + 1) * 128, :], qn.res(h * 128, (h + 1) * 128))
            self.ld(qph, qph.ap[:, :], qpe.ap[h * 64:(h + 1) * 64, :], qpe.res(h * 64, (h + 1) * 64))
            for (t0, tn) in list(lat_blocks) + list(ctx_blocks):
                kts = list(range(NKT)) if t0 < c['NT'] else list(range(NC_ // 128))
                ob, lb = OB[qi % 2], LB[qi % 2]
                rc, osb = RC[qi % 2], OS[qi % 2]
                qi += 1
                sbank = {}
                ptile = {}
                q_n = qnh.ap[:, t0:t0 + tn]
                q_p = qph.ap[0:64, t0:t0 + tn]

                def peA(kt):
                    nonlocal si
                    b = SB[si % 3]
                    sbank[kt] = b
                    ptile[kt] = PT[si % 4]
                    si += 1
                    sps = self.psb(b, 128, tn)
                    k_n = kh.ap[:, kt * 128:(kt + 1) * 128]
                    k_p = kpe.ap[0:64, kt * 128:(kt + 1) * 128]

                    def fn(eng, sps=sps, k_n=k_n, k_p=k_p, q_n=q_n, q_p=q_p):
                        eng.matmul(sps, k_n, q_n, start=True, stop=False)
                        return eng.matmul(sps, k_p, q_p, start=False, stop=True)
                    self.op('pe', fn, kh.res + qnh.res + kpe.res + qph.res, [('ps', b)])
                peA(kts[0])
                for ii, kt in enumerate(kts):
                    if ii + 1 < len(kts):
                        peA(kts[ii + 1])
                    b, pt = sbank[kt], ptile[kt]
                    sps = self.psb(b, 128, tn)
                    p_ap = pt.ap[:, 0:tn]
                    self.op('act', lambda eng, sps=sps, p_ap=p_ap: eng.activation(p_ap, sps, AF.Exp), [('ps', b)], pt.res)
                    o_ps = self.psb(ob, 128, tn)
                    l_ps = self.psb(lb, 128, tn)
                    v_ap = vh.ap[:, kt, :]

                    def fnB(eng, p_ap=p_ap, o_ps=o_ps, l_ps=l_ps, v_ap=v_ap, first=(ii == 0), lastk=(ii == len(kts) - 1)):
                        eng.matmul(o_ps, v_ap, p_ap, start=first, stop=lastk)
                        return eng.matmul(l_ps, onesb.ap[:, :], p_ap, start=first, stop=lastk)
                    self.op('pe', fnB, vh.res + pt.res + onesb.res, [('ps', ob), ('ps', lb)])
                o_ps = self.psb(ob, 128, tn)
                l_ps = self.psb(lb, 128, tn)
                rc_ap = rc.ap[:, 0:tn]
                os_ap = osb.ap[:, 0:tn]
                self.op('dve', lambda eng, rc_ap=rc_ap, l_ps=l_ps: eng.reciprocal(rc_ap, l_ps), [('ps', lb)], rc.res)
                self.op('dve', lambda eng, rc_ap=rc_ap, o_ps=o_ps, os_ap=os_ap: eng.tensor_tensor(os_ap, o_ps, rc_ap, ALU.mult),
                        [('ps', ob)] + rc.res, osb.res)
                r0 = orow + h * 128
                self.st(bcat.ap[r0:r0 + 128, t0:t0 + tn], bcat.res(r0, r0 + 128), osb, os_ap)
        ar.top = mark


def rope_perm(DR):
    j = np.arange(DR)
    return np.where((j % 32) < 16, j + 16, j - 16)


def host_layout(cfg, inp):
    c = cfg
    D, NT, NC_, NTOK, S = c['D'], c['NT'], c['NC'], c['NTOK'], c['S']
    DC = D // 128
    L = c['DEPTH']
    H, DN, DR, DV = c['H'], c['DN'], c['DR'], c['DV']
    CONV, POOL, QL, KVL, DFF = c['CONV'], c['POOL'], c['QL'], c['KVL'], c['DFF']
    f32 = np.float32
    g = {k: np.asarray(v) for k, v in inp.items()}
    perm = rope_perm(DR)

    def fmv(v):
        return np.ascontiguousarray(v.reshape(-1, 128).T)
    o_q = 3 * CONV
    o_kv = o_q + QL
    o_kpe = o_kv + KVL
    o_pool = o_kpe + DR
    o_gate = o_pool + POOL
    full = {}
    for l in range(L):
        wi = g['w_in'][l]
        kpe_cols = wi[:, o_kpe:o_kpe + DR]
        full[('win', l)] = np.concatenate([wi[:, 0:o_q], wi[:, o_q:o_kv], wi[:, o_kv:o_kpe], wi[:, o_pool:o_gate], wi[:, o_gate:],
                                           kpe_cols, kpe_cols[:, perm]], axis=1)
        wq = g['w_uq'][l].reshape(QL, H, DN + DR)
        qpe = wq[:, :, DN:]
        full[('wuq', l)] = np.concatenate([wq[:, :, :DN].reshape(QL, H * DN), qpe.reshape(QL, H * DR), qpe[:, :, perm].reshape(QL, H * DR)], axis=1)
        wkv = g['w_ukv'][l].reshape(KVL, H, DN + DV)
        full[('wukv', l)] = np.concatenate([wkv[:, :, :DN].reshape(KVL, H * DN), wkv[:, :, DN:].reshape(KVL, H * DV)], axis=1)
        full[('wpool', l)] = g['pool_w'][l].reshape(POOL, c['PGD'])
        full[('wcat', l)] = np.concatenate([g['w_conv_out'][l], g['w_mla_out'][l], g['w_pool_out'][l]], axis=0)
        full[('wo', l)] = g['w_o'][l]
        wg = g['w_ffn_gate'][l].reshape(D, DFF // 128, 1, 128)
        wu = g['w_ffn_up'][l].reshape(D, DFF // 128, 1, 128)
        full[('wgu', l)] = np.concatenate([wg, wu], axis=2).reshape(D, 2 * DFF)
        full[('wd', l)] = g['w_ffn_down'][l]
    voff, NV = vec_layout(c)
    coff, NCONST = const_layout(c)
    in_maps = []
    c3 = np.zeros((128, DC, 4), f32)
    c3[:, :, 0] = fmv(g['c'][0])
    c3[:, :, 1] = fmv(g['c'][1])
    c3[:, :, 2] = fmv(g['c_ctx'])
    for r in range(NCORE):
        b, q = r // CPB, r % CPB
        s0 = q * NT
        pos = 2 * (r % CPB) + r // CPB
        m = {}
        m['x_own'] = np.ascontiguousarray(g['x'][b, s0:s0 + NT])
        m['ctx_own'] = np.ascontiguousarray(g['ctx'][b])
        m['c3'] = c3.reshape(128, DC * 4)
        for key, w in full.items():
            K = w.shape[0]
            R = K // NCORE
            rows = w[pos * R:(pos + 1) * R]
            m["%s%d_sh" % key] = np.concatenate([np.ascontiguousarray(rows[:, b0:b0 + bw], dtype=f32).reshape(-1)
                                                 for (b0, bw) in slab_bounds(K, w.shape[1])]).reshape(-1, 2048)
        vecs = np.zeros((128, NV), f32)
        for l in range(L):
            def put(n, v):
                o, w = voff[(n, l)]
                vecs[:, o:o + w] = v
            put('nmw', fmv(g['norm_mix_w'][l]))
            put('nfw', fmv(g['norm_ffn_w'][l]))
            put('qnw', fmv(g['q_norm_w'][l]))
            put('kvnw', fmv(g['kv_norm_w'][l]))
            cw = g['conv_w'][l]
            put('convw', np.ascontiguousarray(cw.reshape(3, CONV // 128, 128).transpose(2, 1, 0)).reshape(128, -1))
            put('pscale', fmv(g['pool_scale'][l]))
            put('bmod', fmv(g['b_mod'][l][pos * c['MODS']:(pos + 1) * c['MODS']]))
            m["wmod%d_sh" % l] = np.ascontiguousarray(g['w_mod'][l][:, pos * c['MODS']:(pos + 1) * c['MODS']], dtype=f32)
        o, w = voff[('fw', 0)]
        vecs[:, o:o + w] = fmv(g['final_norm_w'])
        m['vecs'] = vecs
        cst = np.zeros((128, NCONST), f32)
        toff, NTAB = tab_layout(c)
        tab = np.zeros((128, NTAB), f32)
        t = np.arange(s0, s0 + NT)
        row = (t // c['GRID_W']).astype(f32)
        col = (t % c['GRID_W']).astype(f32)
        axis_dim = DR // 2
        inv_freq = (1.0 / (10000.0 ** (np.arange(0, axis_dim, 2, dtype=f32) / f32(axis_dim)))).astype(f32)
        ang = np.zeros((DR, NT), f32)
        for j in range(DR):
            pos = row if (j // 32) == 0 else col
            ang[j] = pos * inv_freq[j % 16]
        cosT = np.ones((DR, NTOK), f32)
        sinT = np.zeros((DR, NTOK), f32)
        cosT[:, :NT] = np.cos(ang)
        sgn = np.where((np.arange(DR) % 32) < 16, -1.0, 1.0).astype(f32)[:, None]
        sinT[:, :NT] = np.sin(ang) * sgn
        o, w = toff['cos']
        tab[:, o:o + w] = np.tile(cosT, (128 // DR, 1))
        o, w = toff['sin']
        tab[:, o:o + w] = np.tile(sinT, (128 // DR, 1))
        invc = np.zeros((c['PG'], NTOK), f32)
        for gi, wdw in enumerate((2, 4, 8, 16)):
            for (pos, n_seq, off) in ((t, S, 0), (np.arange(NC_), NC_, NT)):
                lo = np.maximum(pos - wdw // 2, 0)
                hi = np.minimum(pos + (wdw - wdw // 2), n_seq)
                invc[gi, off:off + len(pos)] = 1.0 / (hi - lo).astype(f32)
        o, w = toff['invc']
        tab[:, o:o + w] = np.tile(invc.reshape(1, -1), (128, 1))
        m['tabs'] = tab
        sel = np.zeros(2 * NCORE, f32)
        posof = lambda rr: 2 * (rr % CPB) + rr // CPB
        if q > 0:
            sel[posof(r - 1)] = 1.0
        if q < CPB - 1:
            sel[NCORE + posof(r + 1)] = 1.0
        o, w = coff['sel']
        cst[:, o:o + w] = sel[None, :]
        o, w = coff['oh']
        cst[:, o + b] = 1.0
        o, w = coff['ident']
        cst[:, o:o + w] = np.eye(128, dtype=f32)
        m['consts'] = cst
        in_maps.append(m)
    return in_maps


_NC_CACHE = {}


def run(cfg, inputs, trace=False):
    key = tuple(sorted(cfg.items()))
    if key not in _NC_CACHE:
        _NC_CACHE[key] = Builder(cfg).build()
    nc = _NC_CACHE[key]
    in_maps = host_layout(cfg, inputs)
    if cfg.get('FAKEW'):
        in_maps = [{k: v for k, v in m.items() if not k.endswith('_sh')} for m in in_maps]
    res = run_bass_kernel_spmd(nc, in_maps, core_ids=list(range(NCORE)), **({'trace': True} if trace else {}))
    NT = cfg['NT']
    out = np.zeros((cfg['B'], cfg['S'], cfg['D']), np.float32)
    for r in range(NCORE):
        b, q = r // CPB, r % CPB
        out[b, q * NT:(q + 1) * NT] = res.results[r]["y"]
    return out, res


def kernel(**inputs):
    out, _ = run(make_cfg(False), inputs)
    return out
```
